# Optimizing a Trainium2 kernel written in Bass

```python
import jax
import jax.numpy as jnp
from jax import lax
import numpy as np

D_MODEL = 1024
BATCH = 8
SEQ = 4096
DEPTH = 2

HEAD_DIM = 64
ROPE_DIM = HEAD_DIM // 4
ROPE_THETA = 500000.0
Q_BLOCK = 128
NEG_INF = -1e30
RMS_EPS = 1e-6

NSA_HEADS = 4
NSA_CMP_LEN = 32
NSA_CMP_STRIDE = 16
NSA_CMP_HIDDEN = 128
NSA_SEL_LEN = 64
NSA_SEL_TOPK = 16
NSA_WINDOW = 512
NSA_FORCE_SCORE = 1e4

DIL_PATTERNS = ((128, 1), (512, 4), (2048, 16))
DIL_HEADS_PER_GROUP = 2
DIL_HEADS = DIL_HEADS_PER_GROUP * len(DIL_PATTERNS)

FOX_HEADS = 6

D_FF = 2816

NSA_Q_W = NSA_HEADS * HEAD_DIM
NSA_KV_W = 6 * HEAD_DIM
NSA_GATE_W = 3 * NSA_HEADS
DIL_W = DIL_HEADS * HEAD_DIM
DIL_OUT_W = DIL_HEADS_PER_GROUP * HEAD_DIM
FOX_W = FOX_HEADS * HEAD_DIM
IN_WIDTHS = (NSA_Q_W, NSA_KV_W, NSA_GATE_W, 3 * DIL_W, 3 * FOX_W, FOX_HEADS, 3 * D_MODEL)
IN_COLS = sum(IN_WIDTHS)

kernel_name = 'hybrid_nsa_dilated_fox_macaron_block'


def _split_cols(x, widths):
    offsets = [int(o) for o in np.cumsum(widths)[:-1]]
    return jnp.split(x, offsets, axis=-1)


def _rms_norm(x, g):
    xf = x.astype(jnp.float32)
    y = xf * lax.rsqrt(jnp.mean(xf * xf, axis=-1, keepdims=True) + RMS_EPS)
    return (y * g.astype(jnp.float32)).astype(x.dtype)


def _modulate(x, shift, scale):
    return x * (1.0 + scale) + shift


def _swiglu(x, w_in, w_out):
    gate, up = jnp.split(x @ w_in, 2, axis=-1)
    return (jax.nn.silu(gate) * up) @ w_out


def _rope_tables(seq_len):
    inv_freq = ROPE_THETA ** (-jnp.arange(0, ROPE_DIM, 2, dtype=jnp.float32) / ROPE_DIM)
    ang = jnp.arange(seq_len, dtype=jnp.float32)[:, None] * inv_freq[None, :]
    return jnp.cos(ang), jnp.sin(ang)


def _partial_rope(x, cos, sin):
    half = ROPE_DIM // 2
    c = cos[None, :, None, :]
    s = sin[None, :, None, :]
    x1 = x[..., :half].astype(jnp.float32)
    x2 = x[..., half:ROPE_DIM].astype(jnp.float32)
    rot = jnp.concatenate([x1 * c - x2 * s, x2 * c + x1 * s], axis=-1).astype(x.dtype)
    return jnp.concatenate([rot, x[..., ROPE_DIM:]], axis=-1)


def _masked_softmax(s, mask):
    s = jnp.where(mask, s, NEG_INF)
    m = jnp.max(s, axis=-1, keepdims=True)
    e = jnp.where(mask, jnp.exp(s - m), 0.0)
    den = jnp.sum(e, axis=-1, keepdims=True)
    den = jnp.where(den > 0, den, 1.0)
    return e / den, m + jnp.log(den)


def _nsa_attention(q, k_cmp, v_cmp, k_sel, v_sel, k_win, v_win, gate_logits,
                   cmp_pe, cmp_w1, cmp_w2):
    B, S = q.shape[0], q.shape[1]
    scale = HEAD_DIM ** -0.5
    n_cmp = (S - NSA_CMP_LEN) // NSA_CMP_STRIDE + 1
    cmp_start = np.arange(n_cmp) * NSA_CMP_STRIDE
    cmp_idx = cmp_start[:, None] + np.arange(NSA_CMP_LEN)[None, :]

    def compress(t, j):
        blk = t[:, cmp_idx] + cmp_pe[j]
        hid = jax.nn.silu(blk.reshape(B, n_cmp, NSA_CMP_LEN * HEAD_DIM) @ cmp_w1[j])
        return hid @ cmp_w2[j]

    kc = compress(k_cmp, 0)
    vc = compress(v_cmp, 1)
    cmp_end = jnp.asarray(cmp_start + NSA_CMP_LEN - 1)
    n_sel = S // NSA_SEL_LEN
    sel_start = np.arange(n_sel) * NSA_SEL_LEN
    ov = np.minimum(cmp_start[:, None] + NSA_CMP_LEN, sel_start[None, :] + NSA_SEL_LEN) \
        - np.maximum(cmp_start[:, None], sel_start[None, :])
    overlap = jnp.asarray(np.clip(ov, 0, None) / NSA_CMP_LEN, dtype=jnp.float32)
    top_k = min(NSA_SEL_TOPK, n_sel)
    sel_ids = jnp.arange(n_sel)
    kw_pad = jnp.pad(k_win, ((0, 0), (NSA_WINDOW, 0), (0, 0)))
    vw_pad = jnp.pad(v_win, ((0, 0), (NSA_WINDOW, 0), (0, 0)))
    win_len = NSA_WINDOW + Q_BLOCK
    gather = jax.vmap(lambda a, i: a[i])

    def block(qb):
        q0 = qb * Q_BLOCK
        t = q0 + jnp.arange(Q_BLOCK)
        qblk = lax.dynamic_slice_in_dim(q, q0, Q_BLOCK, axis=1)
        s = jnp.einsum('bqhd,bnd->bhqn', qblk, kc).astype(jnp.float32) * scale
        p_cmp, _ = _masked_softmax(s, cmp_end[None, :] <= t[:, None])
        o_cmp = jnp.einsum('bhqn,bnd->bqhd', p_cmp.astype(vc.dtype), vc)
        imp = jnp.einsum('bhqn,nj->bqj', p_cmp, overlap)
        valid = (sel_ids[None, :] * NSA_SEL_LEN) <= t[:, None]
        forced = (sel_ids[None, :] == (t // NSA_SEL_LEN)[:, None]) | (sel_ids[None, :] == 0)
        imp = jnp.where(forced, NSA_FORCE_SCORE, jnp.where(valid, imp, -1.0))
        _, blk_idx = lax.top_k(imp, top_k)
        tok = (blk_idx[..., None] * NSA_SEL_LEN + jnp.arange(NSA_SEL_LEN)).reshape(B, Q_BLOCK * top_k * NSA_SEL_LEN)
        ks = gather(k_sel, tok).reshape(B, Q_BLOCK, top_k * NSA_SEL_LEN, HEAD_DIM)
        vs = gather(v_sel, tok).reshape(B, Q_BLOCK, top_k * NSA_SEL_LEN, HEAD_DIM)
        s = jnp.einsum('bqhd,bqkd->bhqk', qblk, ks).astype(jnp.float32) * scale
        smask = (tok.reshape(B, Q_BLOCK, top_k * NSA_SEL_LEN) <= t[None, :, None])[:, None]
        p, _ = _masked_softmax(s, smask)
        o_sel = jnp.einsum('bhqk,bqkd->bqhd', p.astype(vs.dtype), vs)
        kw = lax.dynamic_slice_in_dim(kw_pad, q0, win_len, axis=1)
        vw = lax.dynamic_slice_in_dim(vw_pad, q0, win_len, axis=1)
        kpos = q0 - NSA_WINDOW + jnp.arange(win_len)
        wmask = (kpos[None, :] <= t[:, None]) & (kpos[None, :] > t[:, None] - NSA_WINDOW) & (kpos[None, :] >= 0)
        s = jnp.einsum('bqhd,bkd->bhqk', qblk, kw).astype(jnp.float32) * scale
        p, _ = _masked_softmax(s, wmask)
        o_win = jnp.einsum('bhqk,bkd->bqhd', p.astype(vw.dtype), vw)
        g = jax.nn.sigmoid(lax.dynamic_slice_in_dim(gate_logits, q0, Q_BLOCK, axis=1).astype(jnp.float32)).astype(q.dtype)
        return g[..., 0:1] * o_cmp + g[..., 1:2] * o_sel + g[..., 2:3] * o_win

    out = lax.map(block, jnp.arange(S // Q_BLOCK))
    return out.transpose(1, 0, 2, 3, 4).reshape(B, S, NSA_Q_W)


def _dilated_attention(q, k, v):
    B, S = q.shape[0], q.shape[1]
    scale = HEAD_DIM ** -0.5
    hpg = DIL_HEADS_PER_GROUP
    qg_all = [q[:, :, g * hpg:(g + 1) * hpg] for g in range(len(DIL_PATTERNS))]
    kg_all = [k[:, :, g * hpg:(g + 1) * hpg] for g in range(len(DIL_PATTERNS))]
    vg_all = [v[:, :, g * hpg:(g + 1) * hpg] for g in range(len(DIL_PATTERNS))]

    def block(qb):
        q0 = qb * Q_BLOCK
        t = q0 + jnp.arange(Q_BLOCK)
        outs, lses = [], []
        for g, (window, dil) in enumerate(DIL_PATTERNS):
            qg = lax.dynamic_slice_in_dim(qg_all[g], q0, Q_BLOCK, axis=1)
            pos = t[:, None] - dil * jnp.arange(window // dil + 1)[None, :]
            valid = pos >= 0
            idx = jnp.maximum(pos, 0)
            kk = kg_all[g][:, idx]
            vv = vg_all[g][:, idx]
            s = jnp.einsum('bqhd,bqkhd->bhqk', qg, kk).astype(jnp.float32) * scale
            p, lse = _masked_softmax(s, valid)
            outs.append(jnp.einsum('bhqk,bqkhd->bqhd', p.astype(vv.dtype), vv))
            lses.append(lse[..., 0])
        wts = jax.nn.softmax(jnp.stack(lses, axis=0), axis=0)
        wts = wts.transpose(0, 1, 3, 2)[..., None].astype(q.dtype)
        return jnp.sum(wts * jnp.stack(outs, axis=0), axis=0)

    out = lax.map(block, jnp.arange(S // Q_BLOCK))
    return out.transpose(1, 0, 2, 3, 4).reshape(B, S, DIL_OUT_W)


def _forgetting_attention(q, k, v, f_logit, f_bias):
    B, S = q.shape[0], q.shape[1]
    scale = HEAD_DIM ** -0.5
    log_f = jax.nn.log_sigmoid(f_logit.astype(jnp.float32) + f_bias.astype(jnp.float32))
    cum = jnp.cumsum(log_f, axis=1)
    cum_k = cum.transpose(0, 2, 1)[:, :, None, :]
    kpos = jnp.arange(S)

    def block(qb):
        q0 = qb * Q_BLOCK
        t = q0 + jnp.arange(Q_BLOCK)
        qblk = lax.dynamic_slice_in_dim(q, q0, Q_BLOCK, axis=1)
        cq = lax.dynamic_slice_in_dim(cum, q0, Q_BLOCK, axis=1).transpose(0, 2, 1)[..., None]
        s = jnp.einsum('bqhd,bkhd->bhqk', qblk, k).astype(jnp.float32) * scale + (cq - cum_k)
        p, _ = _masked_softmax(s, kpos[None, :] <= t[:, None])
        return jnp.einsum('bhqk,bkhd->bqhd', p.astype(v.dtype), v)

    out = lax.map(block, jnp.arange(S // Q_BLOCK))
    return out.transpose(1, 0, 2, 3, 4).reshape(B, S, FOX_W)


def _token_mixing(n, w_in, cmp_pe, cmp_w1, cmp_w2, fox_bias, br_nsa, br_dil, br_fox, w_out, cos, sin):
    B, S = n.shape[0], n.shape[1]
    proj = n @ w_in
    nsa_q, nsa_kv, nsa_g, dil_qkv, fox_qkv, fox_f, merge_g = _split_cols(proj, IN_WIDTHS)

    def heads(t, h):
        return t.reshape(B, S, h, HEAD_DIM)

    def rope_single(t):
        return _partial_rope(t[:, :, None, :], cos, sin)[:, :, 0, :]

    q_a = _partial_rope(heads(nsa_q, NSA_HEADS), cos, sin)
    kc, vc, ks, vs, kw, vw = jnp.split(nsa_kv, 6, axis=-1)
    y_a = _nsa_attention(q_a, rope_single(kc), vc, rope_single(ks), vs, rope_single(kw), vw,
                         nsa_g.reshape(B, S, NSA_HEADS, 3), cmp_pe, cmp_w1, cmp_w2)
    dq, dk, dv = jnp.split(dil_qkv, 3, axis=-1)
    y_b = _dilated_attention(_partial_rope(heads(dq, DIL_HEADS), cos, sin),
                             _partial_rope(heads(dk, DIL_HEADS), cos, sin),
                             heads(dv, DIL_HEADS))
    fq, fk, fv = jnp.split(fox_qkv, 3, axis=-1)
    y_c = _forgetting_attention(heads(fq, FOX_HEADS), heads(fk, FOX_HEADS), heads(fv, FOX_HEADS),
                                fox_f, fox_bias)
    g_a, g_b, g_c = jnp.split(jax.nn.sigmoid(merge_g), 3, axis=-1)
    merged = g_a * (y_a @ br_nsa) + g_b * (y_b @ br_dil) + g_c * (y_c @ br_fox)
    return merged @ w_out


def setup_inputs(seed: int = 0) -> dict:
    key = jax.random.key(seed)
    ks = jax.random.split(key, 17)
    L, D = DEPTH, D_MODEL

    def nrm(k, shape, scale):
        return jax.random.normal(k, shape, jnp.float32) * scale

    return {
        'x': nrm(ks[0], (BATCH, SEQ, D), 1.0),
        'c': nrm(ks[1], (BATCH, D), 1.0),
        'ada_w': nrm(ks[2], (L, D, 9 * D), 0.5 * D ** -0.5),
        'ada_b': nrm(ks[3], (L, 9 * D), 0.02),
        'norm_g': 1.0 + nrm(ks[4], (L, 3, D), 0.05),
        'final_norm_g': 1.0 + nrm(ks[5], (D,), 0.05),
        'ffn_w_in': nrm(ks[6], (L, 2, D, 2 * D_FF), D ** -0.5),
        'ffn_w_out': nrm(ks[7], (L, 2, D_FF, D), D_FF ** -0.5),
        'mix_w_in': nrm(ks[8], (L, D, IN_COLS), D ** -0.5),
        'nsa_cmp_pe': nrm(ks[9], (L, 2, NSA_CMP_LEN, HEAD_DIM), 0.1),
        'nsa_cmp_w1': nrm(ks[10], (L, 2, NSA_CMP_LEN * HEAD_DIM, NSA_CMP_HIDDEN), (NSA_CMP_LEN * HEAD_DIM) ** -0.5),
        'nsa_cmp_w2': nrm(ks[11], (L, 2, NSA_CMP_HIDDEN, HEAD_DIM), NSA_CMP_HIDDEN ** -0.5),
        'fox_f_bias': jax.random.uniform(ks[12], (L, FOX_HEADS), jnp.float32, 1.0, 6.0),
        'br_w_nsa': nrm(ks[13], (L, NSA_Q_W, D), NSA_Q_W ** -0.5),
        'br_w_dil': nrm(ks[14], (L, DIL_OUT_W, D), DIL_OUT_W ** -0.5),
        'br_w_fox': nrm(ks[15], (L, FOX_W, D), FOX_W ** -0.5),
        'mix_w_out': nrm(ks[16], (L, D, D), D ** -0.5),
    }


def reference(x, c, ada_w, ada_b, norm_g, final_norm_g, ffn_w_in, ffn_w_out, mix_w_in,
              nsa_cmp_pe, nsa_cmp_w1, nsa_cmp_w2, fox_f_bias, br_w_nsa, br_w_dil, br_w_fox,
              mix_w_out):
    S = x.shape[1]
    cos, sin = _rope_tables(S)
    cond = jax.nn.silu(c)
    h = x
    for l in range(DEPTH):
        mod = cond @ ada_w[l] + ada_b[l]
        sh1, sc1, ga1, sh2, sc2, ga2, sh3, sc3, ga3 = [m[:, None, :] for m in jnp.split(mod, 9, axis=-1)]
        n = _modulate(_rms_norm(h, norm_g[l, 0]), sh1, sc1)
        h = h + 0.5 * ga1 * _swiglu(n, ffn_w_in[l, 0], ffn_w_out[l, 0])
        n = _modulate(_rms_norm(h, norm_g[l, 1]), sh2, sc2)
        h = h + ga2 * _token_mixing(n, mix_w_in[l], nsa_cmp_pe[l], nsa_cmp_w1[l], nsa_cmp_w2[l],
                                    fox_f_bias[l], br_w_nsa[l], br_w_dil[l], br_w_fox[l],
                                    mix_w_out[l], cos, sin)
        n = _modulate(_rms_norm(h, norm_g[l, 2]), sh3, sc3)
        h = h + 0.5 * ga3 * _swiglu(n, ffn_w_in[l, 1], ffn_w_out[l, 1])
    return _rms_norm(h, final_norm_g)
```

```python
import contextlib
import numpy as np
import ml_dtypes
import concourse.bass as bass
import concourse.mybir as mybir
from concourse.bass_utils import run_bass_kernel_spmd

F32 = mybir.dt.float32
BF16 = mybir.dt.bfloat16
AF = mybir.ActivationFunctionType
ALU = mybir.AluOpType
AX = mybir.AxisListType

D = 1024
DFF = 2816
NEG = -30000.0
EPS = 1e-6
THETA = 500000.0
C_NQ, C_KC, C_VC, C_KS, C_VS, C_KW, C_VW, C_NG = 0, 256, 320, 384, 448, 512, 576, 640
C_DQ, C_DK, C_DV = 652, 1036, 1420
C_FQ, C_FK, C_FV, C_FF, C_MG = 1804, 2188, 2572, 2956, 2962


class Res:
    __slots__ = ("w", "r")

    def __init__(self):
        self.w = None
        self.r = {}


class Sched:
    def __init__(self, nc, es):
        self.nc = nc
        self.engs = {"pe": nc.tensor, "act": nc.scalar, "dve": nc.vector, "pool": nc.gpsimd, "sp": nc.sync}
        self.semobj = {}
        self.cnt = {}
        for k in ("pe", "act", "dve", "pool"):
            self.semobj[k] = es.enter_context(nc.semaphore("c_" + k))
            self.cnt[k] = 0
        self.seen = {k: {} for k in self.engs}
        self.dq = {}
        self.dq_next = {}
        for q, n in (("sp", 20), ("pool", 16), ("act", 6)):
            slots = []
            for i in range(n):
                key = "d_%s%d" % (q, i)
                self.semobj[key] = es.enter_context(nc.semaphore(key))
                slots.append([key, 0])
            self.dq[q] = slots
            self.dq_next[q] = 0
        self.all_tokens = {}

    def _wait(self, e, tok):
        key, val = tok
        if self.seen[e].get(key, 0) >= val:
            return
        self.engs[e].wait_ge(self.semobj[key], val)
        self.seen[e][key] = val

    def _deps(self, e, reads, writes):
        for r in reads:
            if r.w is not None:
                yield r.w
        for w in writes:
            if w.w is not None:
                yield w.w
            for k, v in w.r.items():
                yield (k, v)

    def _note(self, tok):
        k, v = tok
        if self.all_tokens.get(k, 0) < v:
            self.all_tokens[k] = v

    def _commit(self, tok, reads, writes):
        self._note(tok)
        for r in reads:
            if r.r.get(tok[0], 0) < tok[1]:
                r.r[tok[0]] = tok[1]
        for w in writes:
            w.w = tok
            w.r = {}

    def op(self, e, fn, reads=(), writes=(), inc=True):
        for t in list(self._deps(e, reads, writes)):
            if e == "pe" and t[0] == "pe":
                continue
            self._wait(e, t)
        ins = fn(self.engs[e])
        if inc:
            self.cnt[e] += 1
            ins.then_inc(self.semobj[e], 1)
            tok = (e, self.cnt[e])
        else:
            tok = (e, self.cnt[e] + 1)
        self._commit(tok, reads, writes)
        return tok

    def dma(self, q, out, in_, reads=(), writes=(), **kw):
        for t in list(self._deps(q, reads, writes)):
            self._wait(q, t)
        i = self.dq_next[q]
        self.dq_next[q] = (i + 1) % len(self.dq[q])
        slot = self.dq[q][i]
        if slot[1] > 0:
            self._wait(q, (slot[0], slot[1]))
        ins = self.engs[q].dma_start(out=out, in_=in_, **kw)
        slot[1] += 16
        ins.then_inc(self.semobj[slot[0]], 16)
        tok = (slot[0], slot[1])
        self._commit(tok, reads, writes)
        return tok

    def barrier(self):
        for e in ("pe", "act", "dve", "pool", "sp"):
            for k, v in self.all_tokens.items():
                if k == e:
                    continue
                self._wait(e, (k, v))


def mm_group(S, out_ap, out_res, items):
    n = len(items)
    tok = None
    for i, (l, r, rd) in enumerate(items):
        tok = S.op("pe", lambda e, l=l, r=r, i=i: e.matmul(out_ap, l, r, start=(i == 0), stop=(i == n - 1)),
                   reads=rd, writes=[out_res], inc=(i == n - 1))
    return tok


def _bf(a):
    return np.ascontiguousarray(a.astype(ml_dtypes.bfloat16))


def host_constants(S_len):
    T = S_len // 512
    NB = S_len // 128
    k = np.arange(128)[:, None]
    q = np.arange(512)[None, :]
    NC = (S_len - 32) // 16 + 1
    out = {}

    def tile(r, fn):
        dlt = q - (128 * r + k)
        return np.where(fn(dlt), 0.0, NEG).astype(np.float32)

    causal = np.stack([tile(r, lambda d: d >= 0) for r in range(4)])
    win = np.stack([tile(r, lambda d: (d >= 0) & (d < 512)) for r in range(-4, 4)])
    d0 = np.stack([tile(r, lambda d: (d >= 0) & (d <= 128)) for r in range(-1, 4)])
    d1 = np.stack([tile(r, lambda d: (d >= 0) & (d <= 512) & (d % 4 == 0)) for r in range(-4, 4)])
    d2 = np.stack([tile(r, lambda d: (d >= 0) & (d <= 2048) & (d % 16 == 0)) for r in (-16, -15, -14, -13, -12, 0, 1, 2, 3)])
    pm = lambda a: np.ascontiguousarray(a.transpose(1, 0, 2))
    out["m_causal"] = _bf(pm(causal))
    out["m_win"] = _bf(pm(win))
    out["m_d0"] = _bf(pm(d0))
    out["m_d1"] = _bf(pm(d1))
    out["m_d2"] = _bf(pm(d2))
    cm = np.zeros((T, 2, 128, 512), np.float32)
    for i in range(T):
        for c in range(2):
            n = 128 * c + k
            ok = (16 * n + 31 <= 512 * i + q) & (n < NC)
            cm[i, c] = np.where(ok, 0.0, NEG)
    out["m_cmpT"] = _bf(np.ascontiguousarray(cm.reshape(T * 2, 128, 512).transpose(1, 0, 2)))
    p = np.arange(128)[:, None]
    n = np.arange(4 * (S_len // 64))[None, :]
    ctm = np.stack([np.where((16 * n + 31 <= 128 * b + p) & (n < NC), 0.0, NEG) for b in range(NB)])
    out["m_cmpQ"] = _bf(np.ascontiguousarray(ctm.astype(np.float32).transpose(1, 0, 2)))
    E = np.zeros((128, NB, 128), np.float32)
    for kb in range(NB):
        if 2 * kb < 64:
            E[2 * kb, kb, :64] = -NEG
            E[2 * kb + 1, kb, 64:] = -NEG
    out["e_sel"] = _bf(E)
    G = np.zeros((12, 6, 128), np.float32)
    for c in range(2):
        for br in range(3):
            G[3 * (2 * c) + br, c * 3 + br, :64] = 1.0
            G[3 * (2 * c + 1) + br, c * 3 + br, 64:] = 1.0
    out["g_sel"] = G
    inv_freq = (np.float32(THETA) ** (-np.arange(0, 16, 2, dtype=np.float32) / np.float32(16))).astype(np.float32)
    ang = (np.arange(S_len, dtype=np.float32)[:, None] * inv_freq[None, :]).astype(np.float32)
    cos = np.cos(ang).astype(np.float32).T
    sin = np.sin(ang).astype(np.float32).T
    COS = np.ones((128, S_len), np.float32)
    SIN = np.zeros((128, S_len), np.float32)
    for base in (0, 64):
        COS[base:base + 8] = cos
        COS[base + 8:base + 16] = cos
        SIN[base:base + 8] = -sin
        SIN[base + 8:base + 16] = sin
    out["rope_cos"] = COS
    out["rope_sin"] = SIN
    out["ident_bf"] = _bf(np.eye(128, dtype=np.float32))
    out["ident_f"] = np.eye(128, dtype=np.float32)
    return out


CONST_SHAPES = None


def build_program(S_len, consts, n_layers=2, stop_after=None, do_mixers=("fox", "dil", "nsa"), do_merge=True, dbg=False, nsa_stop=None):
    T = S_len // 512
    NB = S_len // 128
    nc = bass.Bass("TRN2", target_bir_lowering=False)
    es = contextlib.ExitStack()
    with es:
        S = Sched(nc, es)

        def din(name, shape, dt=F32):
            return nc.dram_tensor(name, list(shape), dt, kind="ExternalInput").ap()

        def dscr(name, shape, dt=F32):
            return nc.dram_tensor(name, list(shape), dt).ap()

        x_d = din("x", [S_len, D])
        c_d = din("c", [128, 8])
        ada_w_d = din("ada_w", [n_layers, D, 9 * D])
        ada_b_d = din("ada_b", [n_layers, 128, 72])
        norm_g_d = din("norm_g", [n_layers, 3, 128, 8])
        fin_g_d = din("final_norm_g", [128, 8])
        w_in_d = din("ffn_w_in", [n_layers, 2, D, 2 * DFF])
        w_out_d = din("ffn_w_out", [n_layers, 2, DFF, D])
        mix_in_d = din("mix_w_in", [n_layers, D, 6034])
        mix_sw_d = din("mix_w_swap", [n_layers, D, 19 * 64])
        pe_d = din("nsa_cmp_peT", [n_layers, 2, 64, 32])
        w1_d = din("nsa_cmp_w1", [n_layers, 2, 2048, 128])
        w2_d = din("nsa_cmp_w2", [n_layers, 2, 128, 64])
        fb_d = din("fox_f_bias", [n_layers, 128, 6])
        brn_d = din("br_w_nsa", [n_layers, 256, D])
        brd_d = din("br_w_dil", [n_layers, 128, D])
        brf_d = din("br_w_fox", [n_layers, 384, D])
        mo_d = din("mix_w_out", [n_layers, D, D])
        cd = {}
        for k, v in consts.items():
            cd[k] = din(k, v.shape, BF16 if v.dtype == ml_dtypes.bfloat16 else F32)
        out_d = nc.dram_tensor("out", [S_len, D], F32, kind="ExternalOutput").ap()

        hT_d = dscr("hT", [D, S_len])
        n2_d = dscr("n2T", [D, S_len], BF16)
        if dbg:
            yT_d = nc.dram_tensor("dbg_y", [768, S_len], BF16, kind="ExternalOutput").ap()
        else:
            yT_d = dscr("yT", [768, S_len], BF16)

        uniq = [0]

        def sb(name, shape, dt=F32, stack=es):
            uniq[0] += 1
            return stack.enter_context(nc.sbuf_tensor("s%d_%s" % (uniq[0], name), list(shape), dt))

        ident_bf = sb("ident_bf", [128, 128], BF16)
        ident_f = sb("ident_f", [128, 128], F32)
        ones_bf = sb("ones_bf", [128, 128], BF16)
        modT = sb("modT", [128, 72], F32)
        gs = sb("gs", [128, 3, 8], F32)
        hga = sb("hga", [128, 3, 8], F32)
        cond = sb("cond", [128, 8], BF16)
        smalls = sb("smalls", [128, 64], F32)
        R_const = Res()
        R_mod = Res()
        banks = [es.enter_context(nc.psum_tensor("bank%d" % i, [128, 512], F32)) for i in range(7)]
        bankT = es.enter_context(nc.psum_tensor("bankT", [128, 1024], BF16))
        R_bank = [Res() for _ in range(7)]
        R_bankT = Res()

        S.dma("sp", ident_bf[:], cd["ident_bf"], writes=[R_const])
        S.dma("sp", ident_f[:], cd["ident_f"], writes=[R_const])
        S.op("dve", lambda e: e.memset(ones_bf[:], 1.0 / 1024.0), writes=[R_const])
        ctmp = sb("ctmp", [128, 8], F32)
        R_c = Res()
        S.dma("sp", ctmp[:], c_d, writes=[R_c])
        S.op("act", lambda e: e.activation(out=cond[:], in_=ctmp[:], func=AF.Silu), reads=[R_c], writes=[R_const])

        def compute_mod(l):
            with contextlib.ExitStack() as ps:
                wb = [sb("adaw%d" % i, [128, 8, 1024], BF16, ps) for i in range(2)]
                Rw = [Res(), Res()]
                adab = sb("adab", [128, 72], F32, ps)
                ng = sb("ng", [128, 3, 8], F32, ps)
                Rl = Res()
                S.dma("sp", adab[:], ada_b_d[l], writes=[Rl])
                S.dma("sp", ng[:], norm_g_d[l].rearrange("i p k -> p i k"), writes=[Rl])
                for g in range(9):
                    b = g % 2
                    for kc in range(8):
                        S.dma("pool", wb[b][:, kc, :], ada_w_d[l, kc * 128:(kc + 1) * 128, g * 1024:(g + 1) * 1024],
                              writes=[Rw[b]])
                    for jj in range(8):
                        j = g * 8 + jj
                        bk = jj % 2
                        mm_group(S, banks[bk][:, 0:1], R_bank[bk],
                                 [(wb[b][:, kc, jj * 128:(jj + 1) * 128], cond[:, kc:kc + 1], [Rw[b], R_const])
                                  for kc in range(8)])
                        S.op("dve", lambda e, j=j, bk=bk: e.tensor_tensor(out=modT[:, j:j + 1], in0=banks[bk][:, 0:1],
                                                                          in1=adab[:, j:j + 1], op=ALU.add),
                             reads=[R_bank[bk], Rl], writes=[R_mod])
                for i in range(3):
                    sc = modT[:, (3 * i + 1) * 8:(3 * i + 2) * 8]
                    ga = modT[:, (3 * i + 2) * 8:(3 * i + 3) * 8]
                    S.op("dve", lambda e, i=i, sc=sc: e.scalar_tensor_tensor(out=gs[:, i, :], in0=sc, scalar=1.0,
                                                                             in1=ng[:, i, :], op0=ALU.add, op1=ALU.mult),
                         reads=[R_mod, Rl], writes=[R_mod])
                    S.op("dve", lambda e, i=i, ga=ga: e.tensor_scalar(out=hga[:, i, :], in0=ga,
                                                                      scalar1=(1.0 if i == 1 else 0.5), scalar2=None,
                                                                      op0=ALU.mult),
                         reads=[R_mod], writes=[R_mod])
                S.barrier()

        def norm_tile(ps, h_t, R_h, n_t, R_n, gs_ap, sh_ap, wk):
            sq, R_sq, rstd, R_rstd, tmp, R_tmp = wk
            items = []
            for kc in range(8):
                b = kc % 2
                S.op("act", lambda e, kc=kc, b=b: e.activation(out=sq[b][:], in_=h_t[:, kc, :], func=AF.Square),
                     reads=[R_h], writes=[R_sq[b]])
                S.op("pe", lambda e, kc=kc, b=b: e.matmul(banks[6][:], ones_bf[:], sq[b][:], start=(kc == 0),
                                                          stop=(kc == 7)),
                     reads=[R_sq[b], R_const], writes=[R_bank[6]], inc=True)
            S.op("act", lambda e: e.activation(out=rstd[:], in_=banks[6][:], func=AF.Sqrt, bias=eps_t[:], scale=1.0),
                 reads=[R_bank[6], R_const], writes=[R_rstd])
            S.op("dve", lambda e: e.reciprocal(out=rstd[:], in_=rstd[:]), reads=[R_rstd], writes=[R_rstd])
            for kc in range(8):
                b = kc % 2
                S.op("dve", lambda e, kc=kc, b=b: e.tensor_tensor(out=tmp[b][:], in0=h_t[:, kc, :], in1=rstd[:],
                                                                  op=ALU.mult),
                     reads=[R_h, R_rstd], writes=[R_tmp[b]])
                S.op("act", lambda e, kc=kc, b=b: e.activation(out=n_t[:, kc, :], in_=tmp[b][:], func=AF.Identity,
                                                               scale=gs_ap[:, kc:kc + 1], bias=sh_ap[:, kc:kc + 1]),
                     reads=[R_tmp[b], R_mod], writes=[R_n])

        def norm_work(ps):
            sq = [sb("sq%d" % i, [128, 512], BF16, ps) for i in range(2)]
            rstd = sb("rstd", [128, 512], F32, ps)
            tmp = [sb("ntmp%d" % i, [128, 512], F32, ps) for i in range(2)]
            return (sq, [Res(), Res()], rstd, Res(), tmp, [Res(), Res()])

        fill_big = nc.gpsimd.to_reg(1e4)
        fill_neg1 = nc.gpsimd.to_reg(-1.0)
        fill_zero = nc.gpsimd.to_reg(0.0)
        eps_t = sb("eps_t", [128, 1], F32)
        S.op("dve", lambda e: e.memset(eps_t[:], EPS), writes=[R_const])

        R_hT = [Res() for _ in range(T)]
        R_n2 = [Res() for _ in range(T)]
        R_y = Res()

        def hT_tile(t):
            return hT_d[:, t * 512:(t + 1) * 512].rearrange("(k p) t -> p k t", p=128)

        def phase_transpose_in():
            with contextlib.ExitStack() as ps:
                xin = [sb("xin%d" % i, [128, 4, D], F32, ps) for i in range(2)]
                Rx = [Res(), Res()]
                ho = [sb("ho%d" % i, [128, 8, 512], F32, ps) for i in range(2)]
                Rho = [Res(), Res()]
                for t in range(T):
                    b = t % 2
                    S.dma("sp", xin[b][:], x_d[t * 512:(t + 1) * 512, :].rearrange("(a p) d -> p a d", p=128),
                          writes=[Rx[b]])
                    for kc in range(8):
                        bk = kc % 4
                        for a in range(4):
                            S.op("pe", lambda e, a=a, kc=kc, bk=bk, b=b: e.transpose(
                                out=banks[bk][:, a * 128:(a + 1) * 128], in_=xin[b][:, a, kc * 128:(kc + 1) * 128],
                                identity=ident_f[:]), reads=[Rx[b], R_const], writes=[R_bank[bk]], inc=(a == 3))
                        eng = "dve" if kc % 2 == 0 else "act"
                        if eng == "dve":
                            S.op("dve", lambda e, kc=kc, bk=bk, b=b: e.tensor_copy(out=ho[b][:, kc, :], in_=banks[bk][:]),
                                 reads=[R_bank[bk]], writes=[Rho[b]])
                        else:
                            S.op("act", lambda e, kc=kc, bk=bk, b=b: e.activation(out=ho[b][:, kc, :], in_=banks[bk][:],
                                                                                  func=AF.Copy),
                                 reads=[R_bank[bk]], writes=[Rho[b]])
                    S.dma("sp", hT_tile(t), ho[b][:], reads=[Rho[b]], writes=[R_hT[t]])
                S.barrier()

        def phase_final():
            with contextlib.ExitStack() as ps:
                fg = sb("fg", [128, 8], F32, ps)
                zero8 = sb("zero8", [128, 8], F32, ps)
                Rf = Res()
                S.dma("sp", fg[:], fin_g_d, writes=[Rf])
                S.op("dve", lambda e: e.memset(zero8[:], 0.0), writes=[Rf])
                S.op("dve", lambda e: e.tensor_copy(out=gs[:, 0, :], in_=fg[:]), reads=[Rf], writes=[R_mod])
                hin = [sb("fh%d" % i, [128, 8, 512], F32, ps) for i in range(2)]
                Rh = [Res(), Res()]
                nn = [sb("fn%d" % i, [128, 8, 512], F32, ps) for i in range(2)]
                Rn = [Res(), Res()]
                ot = [sb("fo%d" % i, [128, 4, D], F32, ps) for i in range(2)]
                Ro = [Res(), Res()]
                wk = norm_work(ps)
                toks = []
                for t in range(T):
                    b = t % 2
                    S.dma("sp", hin[b][:], hT_tile(t), reads=[R_hT[t]], writes=[Rh[b]])
                    norm_tile(ps, hin[b], Rh[b], nn[b], Rn[b], gs[:, 0, :], zero8, wk)
                    for a in range(4):
                        for kc in range(8):
                            bk = kc % 4
                            S.op("pe", lambda e, a=a, kc=kc, bk=bk, b=b: e.transpose(
                                out=banks[bk][:, 0:128], in_=nn[b][:, kc, a * 128:(a + 1) * 128], identity=ident_f[:]),
                                reads=[Rn[b], R_const], writes=[R_bank[bk]])
                            if kc % 2 == 0:
                                S.op("dve", lambda e, a=a, kc=kc, bk=bk, b=b: e.tensor_copy(
                                    out=ot[b][:, a, kc * 128:(kc + 1) * 128], in_=banks[bk][:, 0:128]),
                                    reads=[R_bank[bk]], writes=[Ro[b]])
                            else:
                                S.op("act", lambda e, a=a, kc=kc, bk=bk, b=b: e.activation(
                                    out=ot[b][:, a, kc * 128:(kc + 1) * 128], in_=banks[bk][:, 0:128], func=AF.Copy),
                                    reads=[R_bank[bk]], writes=[Ro[b]])
                    toks.append(S.dma("sp", out_d[t * 512:(t + 1) * 512, :].rearrange("(a p) d -> p a d", p=128),
                                      ot[b][:], reads=[Ro[b]]))
                for tk in toks:
                    S._wait("sp", tk)
                S.barrier()

        def phase_ffn(l, which):
            ni = 0 if which == 0 else 2
            with contextlib.ExitStack() as ps:
                w_in = sb("w_in", [128, 8, 2 * DFF], BF16, ps)
                w_out = sb("w_out", [128, 22, D], BF16, ps)
                Rw = Res()
                for kc in range(8):
                    S.dma("pool", w_in[:, kc, :], w_in_d[l, which, kc * 128:(kc + 1) * 128, :], writes=[Rw])
                for j in range(22):
                    S.dma("pool", w_out[:, j, :], w_out_d[l, which, j * 128:(j + 1) * 128, :], writes=[Rw])
                h_t = sb("h_t", [128, 8, 512], F32, ps)
                R_h = Res()
                n_t = sb("n_t", [128, 8, 512], BF16, ps)
                R_n = Res()
                hid = sb("hid", [128, 22, 512], BF16, ps)
                R_hid = [Res() for _ in range(22)]
                sg = [sb("sg%d" % i, [128, 512], BF16, ps) for i in range(2)]
                R_sg = [Res(), Res()]
                hr = [sb("hr%d" % i, [128, 512], F32, ps) for i in range(2)]
                R_hr = [Res(), Res()]
                hn = [sb("hn%d" % i, [128, 512], F32, ps) for i in range(2)]
                R_hn = [Res(), Res()]
                wk = norm_work(ps)
                sh_ap = modT[:, (3 * ni) * 8:(3 * ni + 1) * 8]
                for t in range(T):
                    S.dma("sp", h_t[:], hT_tile(t), reads=[R_hT[t]], writes=[R_h])
                    norm_tile(ps, h_t, R_h, n_t, R_n, gs[:, ni, :], sh_ap, wk)
                    for j in range(22):
                        bg = j % 2
                        bu = 2 + (j % 2)
                        mm_group(S, banks[bg][:], R_bank[bg],
                                 [(w_in[:, kc, j * 128:(j + 1) * 128], n_t[:, kc, :], [Rw, R_n]) for kc in range(8)])
                        mm_group(S, banks[bu][:], R_bank[bu],
                                 [(w_in[:, kc, DFF + j * 128:DFF + (j + 1) * 128], n_t[:, kc, :], [Rw, R_n])
                                  for kc in range(8)])
                        S.op("act", lambda e, bg=bg: e.activation(out=sg[bg][:], in_=banks[bg][:], func=AF.Silu),
                             reads=[R_bank[bg]], writes=[R_sg[bg]])
                        S.op("dve", lambda e, j=j, bg=bg, bu=bu: e.tensor_tensor(out=hid[:, j, :], in0=sg[bg][:],
                                                                                 in1=banks[bu][:], op=ALU.mult),
                             reads=[R_sg[bg], R_bank[bu]], writes=[R_hid[j]])
                    for oc in range(8):
                        bd = 4 + (oc % 2)
                        b = oc % 2
                        S.dma("sp", hr[b][:], hT_d[oc * 128:(oc + 1) * 128, t * 512:(t + 1) * 512],
                              reads=[R_hT[t]], writes=[R_hr[b]])
                        mm_group(S, banks[bd][:], R_bank[bd],
                                 [(w_out[:, j, oc * 128:(oc + 1) * 128], hid[:, j, :], [Rw, R_hid[j]]) for j in range(22)])
                        S.op("dve", lambda e, oc=oc, bd=bd, b=b: e.scalar_tensor_tensor(
                            out=hn[b][:], in0=banks[bd][:], scalar=hga[:, ni, oc:oc + 1], in1=hr[b][:],
                            op0=ALU.mult, op1=ALU.add),
                            reads=[R_bank[bd], R_hr[b], R_mod], writes=[R_hn[b]])
                        S.dma("sp", hT_d[oc * 128:(oc + 1) * 128, t * 512:(t + 1) * 512], hn[b][:],
                              reads=[R_hn[b]], writes=[R_hT[t]])
                S.barrier()

        def n2_tile(t):
            return n2_d[:, t * 512:(t + 1) * 512].rearrange("(k p) t -> p k t", p=128)

        def phase_n2():
            with contextlib.ExitStack() as ps:
                hin = [sb("n2h%d" % i, [128, 8, 512], F32, ps) for i in range(2)]
                Rh = [Res(), Res()]
                nn = [sb("n2n%d" % i, [128, 8, 512], BF16, ps) for i in range(2)]
                Rn = [Res(), Res()]
                wk = norm_work(ps)
                sh_ap = modT[:, 24:32]
                for t in range(T):
                    b = t % 2
                    S.dma("sp", hin[b][:], hT_tile(t), reads=[R_hT[t]], writes=[Rh[b]])
                    norm_tile(ps, hin[b], Rh[b], nn[b], Rn[b], gs[:, 1, :], sh_ap, wk)
                    S.dma("sp", n2_tile(t), nn[b][:], reads=[Rn[b]], writes=[R_n2[t]])
                S.barrier()

        def wload(dst, src2d, c0, n, Rw):
            for kc in range(8):
                S.dma("pool", dst[:, kc, :], src2d[kc * 128:(kc + 1) * 128, c0:c0 + n], writes=[Rw])

        class AttnState:
            def __init__(self, ps):
                self.P = [sb("Pt%d" % i, [128, 512], BF16, ps) for i in range(3)]
                self.RP = [Res() for _ in range(3)]
                self.rd = [sb("rd%d" % i, [128, 512], F32, ps) for i in range(2)]
                self.Rrd = [Res(), Res()]
                self.ys = [sb("ys%d" % i, [128, 512], BF16, ps) for i in range(2)]
                self.Rys = [Res(), Res()]
                self.si = 0
                self.pi = 0
                self.oi = 0
                self.ri = 0

        def run_units(st, units):
            LA = 2
            n = len(units)
            for idx in range(n + LA):
                if idx < n:
                    u = units[idx]
                    sbk = st.si % 3
                    st.si += 1
                    u["sbk"] = sbk
                    mm_group(S, banks[sbk][:], R_bank[sbk], [(u["k"], u["q"], u["rdk"])] + u["masks"])
                j = idx - LA
                if j >= 0:
                    u = units[j]
                    pb = st.pi % 3
                    st.pi += 1
                    sbk = u["sbk"]
                    bias = u["bias"]
                    if bias is None:
                        S.op("act", lambda e, pb=pb, sbk=sbk: e.activation(out=st.P[pb][:], in_=banks[sbk][:], func=AF.Exp),
                             reads=[R_bank[sbk]], writes=[st.RP[pb]])
                    else:
                        S.op("act", lambda e, pb=pb, sbk=sbk, bias=bias: e.activation(out=st.P[pb][:], in_=banks[sbk][:],
                                                                                    func=AF.Exp, bias=bias, scale=1.0),
                             reads=[R_bank[sbk]] + u["rdb"], writes=[st.RP[pb]])
                    ob = u["ob"]
                    S.op("pe", lambda e, u=u, pb=pb, ob=ob: e.matmul(banks[ob][:], u["v"], st.P[pb][:], start=u["first"],
                                                                     stop=u["last"]),
                         reads=[st.RP[pb]] + u["rdv"], writes=[R_bank[ob]])
                    if u["last"] and u["fin"] is not None:
                        u["fin"]()

        def std_finalize(st, ob, variant, dst_ap_fn, R_dst, clamp=False):
            nr = slice(0, 64) if variant == "A" else slice(64, 128)
            dr = slice(64, 128) if variant == "A" else slice(0, 64)
            ri = st.ri % 2
            st.ri += 1
            rd = st.rd[ri]
            if clamp:
                S.op("dve", lambda e: e.tensor_scalar(out=rd[nr, :], in0=banks[ob][dr, :], scalar1=1e-30, scalar2=None,
                                                      op0=ALU.max), reads=[R_bank[ob]], writes=[st.Rrd[ri]])
                S.op("dve", lambda e: e.reciprocal(out=rd[nr, :], in_=rd[nr, :]), reads=[st.Rrd[ri]], writes=[st.Rrd[ri]])
            else:
                S.op("dve", lambda e: e.reciprocal(out=rd[nr, :], in_=banks[ob][dr, :]), reads=[R_bank[ob]],
                     writes=[st.Rrd[ri]])
            return nr, rd, ri

        def finalize_to_dram(st, ob, dram_ap, clamp=False):
            nr, rd, ri = std_finalize(st, ob, "A", None, None, clamp)
            ys = st.ys[ri]
            S.op("dve", lambda e: e.tensor_tensor(out=ys[0:64, :], in0=banks[ob][0:64, :], in1=rd[0:64, :], op=ALU.mult),
                 reads=[R_bank[ob], st.Rrd[ri]], writes=[st.Rys[ri]])
            S.dma("sp", dram_ap, ys[0:64, :], reads=[st.Rys[ri]], writes=[R_y])

        def phase_mixer(l):
            with contextlib.ExitStack() as ms:
                mcaus = sb("mcaus", [128, 4, 512], BF16, ms)
                R_mc = Res()
                S.dma("sp", mcaus[:], cd["m_causal"], writes=[R_mc])
                if "fox" in do_mixers:
                    mixer_fox(l, mcaus, R_mc)
                if "dil" in do_mixers:
                    mixer_dil(l)
                if "nsa" in do_mixers:
                    mixer_nsa(l, mcaus, R_mc)
                S.barrier()
            if do_merge:
                mixer_merge(l)

        def mixer_fox(l, mcaus, R_mc):
            with contextlib.ExitStack() as ps:
                wf = sb("wfox", [128, 8, 1158], BF16, ps)
                Rw = Res()
                wload(wf, mix_in_d[l], C_FQ, 1158, Rw)
                fb = sb("fb", [128, 6], F32, ps)
                S.dma("sp", fb[:], fb_d[l], writes=[Rw])
                tri = sb("tri", [128, 128], F32, ps)
                onesf = sb("onesf", [128, 128], F32, ps)
                R_k = Res()
                S.op("pool", lambda e: e.memset(onesf[:], 1.0), writes=[R_k])
                S.op("pool", lambda e: e.affine_select(out=tri[:], in_=onesf[:], pattern=[[1, 128]], compare_op=ALU.is_ge,
                                                       fill=fill_zero, base=0, channel_multiplier=-1),
                     reads=[R_k], writes=[R_k])
                logf = sb("logf", [128, NB, 6], F32, ps)
                pref = sb("pref", [128, NB, 6], F32, ps)
                negcum = sb("negcum", [128, NB, 6], F32, ps)
                cumrow = sb("cumrow", [6, S_len], BF16, ps)
                R_lf, R_pref, R_nc, R_cr = Res(), Res(), Res(), Res()
                nt = [sb("fxn%d" % i, [128, 8, 512], BF16, ps) for i in range(2)]
                Rn = [Res(), Res()]
                zt = sb("fz", [128, 4, 6], F32, ps)
                R_z = Res()
                for t in range(T):
                    b = t % 2
                    S.dma("sp", nt[b][:], n2_tile(t), reads=[R_n2[t]], writes=[Rn[b]])
                    for a in range(4):
                        mm_group(S, banks[5][:, a * 8:a * 8 + 6], R_bank[5],
                                 [(nt[b][:, kc, a * 128:(a + 1) * 128], wf[:, kc, 1152:1158], [Rn[b], Rw]) for kc in range(8)])
                    for a in range(4):
                        S.op("dve", lambda e, a=a: e.tensor_tensor(out=zt[:, a, :], in0=banks[5][:, a * 8:a * 8 + 6],
                                                                   in1=fb[:], op=ALU.add),
                             reads=[R_bank[5], Rw], writes=[R_z])
                    S.op("act", lambda e: e.activation(out=zt[:], in_=zt[:], func=AF.Exp, scale=-1.0), reads=[R_z], writes=[R_z])
                    S.op("act", lambda e, t=t: e.activation(out=logf[:, t * 4:(t + 1) * 4, :], in_=zt[:], func=AF.Ln, bias=1.0,
                                                            scale=1.0), reads=[R_z], writes=[R_lf])
                S.op("dve", lambda e: e.tensor_scalar(out=logf[:], in0=logf[:], scalar1=-1.0, scalar2=None, op0=ALU.mult),
                     reads=[R_lf], writes=[R_lf])
                S.op("dve", lambda e: e.memset(pref[:, 0, :], 0.0), writes=[R_pref])
                for b in range(1, NB):
                    S.op("dve", lambda e, b=b: e.tensor_tensor(out=pref[:, b, :], in0=pref[:, b - 1, :], in1=logf[:, b - 1, :],
                                                               op=ALU.add), reads=[R_lf, R_pref], writes=[R_pref])
                for b0 in range(0, NB, 64):
                    nbb = min(64, NB - b0)
                    for b in range(b0, b0 + nbb):
                        mm_group(S, banks[5][:, (b - b0) * 8:(b - b0) * 8 + 6], R_bank[5],
                                 [(tri[:], logf[:, b, :], [R_k, R_lf]), (onesf[:], pref[:, b, :], [R_k, R_pref])])
                    S.op("dve", lambda e, b0=b0, nbb=nbb: e.tensor_scalar(
                        out=negcum[:, b0:b0 + nbb, :], in0=banks[5][:, 0:nbb * 8].rearrange("p (b e) -> p b e", e=8)[:, :, 0:6],
                        scalar1=-1.0, scalar2=None, op0=ALU.mult), reads=[R_bank[5]], writes=[R_nc])
                for b0 in range(0, NB, 4):
                    for b in range(b0, b0 + 4):
                        S.op("pe", lambda e, b=b, b0=b0: e.transpose(out=banks[6][0:6, (b - b0) * 128:(b - b0 + 1) * 128],
                                                                     in_=negcum[:, b, :], identity=ident_f[:]),
                             reads=[R_nc, R_const], writes=[R_bank[6]], inc=(b == b0 + 3))
                    S.op("dve", lambda e, b0=b0: e.tensor_scalar(out=cumrow[:, b0 * 128:(b0 + 4) * 128], in0=banks[6][0:6, :],
                                                                 scalar1=-1.0, scalar2=None, op0=ALU.mult),
                         reads=[R_bank[6]], writes=[R_cr])
                Qh = sb("fQ", [128, S_len], BF16, ps)
                Kh = sb("fK", [128, S_len], BF16, ps)
                Vh = sb("fV", [128, NB, 192], BF16, ps)
                R_Q, R_K, R_V = Res(), Res(), Res()
                S.op("pool", lambda e: e.memset(Vh[:], 1.0), writes=[R_V])
                S.op("pool", lambda e: e.memset(Kh[64:65, :], 1.0), writes=[R_K])
                st = AttnState(ps)
                for h in range(6):
                    S.dma("sp", Qh[64:65, :], cumrow[h:h + 1, :], reads=[R_cr], writes=[R_Q])
                    for t in range(T):
                        b = t % 2
                        S.dma("sp", nt[b][:], n2_tile(t), reads=[R_n2[t]], writes=[Rn[b]])
                        mm_group(S, banks[5][0:64, :], R_bank[5],
                                 [(wf[:, kc, 64 * h:64 * h + 64], nt[b][:, kc, :], [Rw, Rn[b]]) for kc in range(8)])
                        S.op("act", lambda e, t=t: e.activation(out=Qh[0:64, t * 512:(t + 1) * 512], in_=banks[5][0:64, :],
                                                                func=AF.Copy, scale=0.125), reads=[R_bank[5]], writes=[R_Q])
                        mm_group(S, banks[6][0:64, :], R_bank[6],
                                 [(wf[:, kc, 384 + 64 * h:384 + 64 * h + 64], nt[b][:, kc, :], [Rw, Rn[b]]) for kc in range(8)])
                        S.op("dve", lambda e, t=t: e.tensor_copy(out=Kh[0:64, t * 512:(t + 1) * 512], in_=banks[6][0:64, :]),
                             reads=[R_bank[6]], writes=[R_K])
                        for a in range(4):
                            mm_group(S, banks[5][:, a * 64:(a + 1) * 64], R_bank[5],
                                     [(nt[b][:, kc, a * 128:(a + 1) * 128], wf[:, kc, 768 + 64 * h:768 + 64 * h + 64],
                                       [Rn[b], Rw]) for kc in range(8)])
                        S.op("dve", lambda e, t=t: e.tensor_copy(
                            out=Vh[:, t * 4:(t + 1) * 4, 64:128], in_=banks[5][:, 0:256].rearrange("p (a d) -> p a d", d=64)),
                            reads=[R_bank[5]], writes=[R_V])
                    units = []
                    for i in range(T):
                        ob = 3 + (st.oi % 2)
                        st.oi += 1
                        nkb = 4 * i + 4
                        for kb in range(nkb):
                            masks = []
                            if kb >= 4 * i:
                                masks = [(ident_bf[:], mcaus[:, kb - 4 * i, :], [R_const, R_mc])]
                            v_ap = Vh[:, kb, 64:192]

                            def fin(i=i, ob=ob, h=h):
                                finalize_to_dram(st, ob, yT_d[384 + 64 * h:448 + 64 * h, i * 512:(i + 1) * 512])
                            units.append(dict(k=Kh[0:65, kb * 128:(kb + 1) * 128], q=Qh[0:65, i * 512:(i + 1) * 512],
                                              rdk=[R_K, R_Q], masks=masks, bias=negcum[:, kb, h:h + 1], rdb=[R_nc],
                                              v=v_ap, rdv=[R_V], ob=ob, first=(kb == 0), last=(kb == nkb - 1), fin=fin))
                    run_units(st, units)
                S.barrier()

        def rope_evac(bA, bB, cos_t, sin_t, R_cs, dst_ap, R_dst, scale, tt, R_tt):
            S.op("dve", lambda e: e.scalar_tensor_tensor(out=tt[0][:], in0=banks[bB][:], scalar=scale, in1=sin_t,
                                                         op0=ALU.mult, op1=ALU.mult),
                 reads=[R_bank[bB], R_cs], writes=[R_tt[0]])
            S.op("dve", lambda e: e.scalar_tensor_tensor(out=tt[1][:], in0=banks[bA][:], scalar=scale, in1=cos_t,
                                                         op0=ALU.mult, op1=ALU.mult),
                 reads=[R_bank[bA], R_cs], writes=[R_tt[1]])
            S.op("pool", lambda e: e.tensor_tensor(out=dst_ap, in0=tt[0][:], in1=tt[1][:], op=ALU.add),
                 reads=[R_tt[0], R_tt[1]], writes=[R_dst])

        class ProjWork:
            def __init__(self, ps):
                self.nt = [sb("pn%d" % i, [128, 8, 512], BF16, ps) for i in range(2)]
                self.Rn = [Res(), Res()]
                self.cs = [sb("pcs%d" % i, [128, 2, 512], F32, ps) for i in range(2)]
                self.Rcs = [Res(), Res()]
                self.tt = [[sb("ptt%d%d" % (i, j), [128, 512], F32, ps) for j in range(2)] for i in range(2)]
                self.Rtt = [[Res(), Res()], [Res(), Res()]]
                self.ci = 0

            def load(self, t):
                b = t % 2
                S.dma("sp", self.nt[b][:], n2_tile(t), reads=[R_n2[t]], writes=[self.Rn[b]])
                S.dma("sp", self.cs[b][:, 0, :], cd["rope_cos"][:, t * 512:(t + 1) * 512], writes=[self.Rcs[b]])
                S.dma("sp", self.cs[b][:, 1, :], cd["rope_sin"][:, t * 512:(t + 1) * 512], writes=[self.Rcs[b]])
                return b

            def roped(self, b, w_main, w_swap, Rw, dst_ap, R_dst, scale):
                p = self.ci % 2
                self.ci += 1
                bA, bB = 2 * p, 2 * p + 1
                mm_group(S, banks[bA][:], R_bank[bA], [(w_main(kc), self.nt[b][:, kc, :], [Rw, self.Rn[b]]) for kc in range(8)])
                mm_group(S, banks[bB][:], R_bank[bB], [(w_swap(kc), self.nt[b][:, kc, :], [Rw, self.Rn[b]]) for kc in range(8)])
                rope_evac(bA, bB, self.cs[b][:, 0, :], self.cs[b][:, 1, :], self.Rcs[b], dst_ap, R_dst, scale,
                          self.tt[p], self.Rtt[p])

        def mixer_dil(l):
            with contextlib.ExitStack() as ps:
                Qg = [sb("dQ%d" % g, [128, S_len], BF16, ps) for g in range(3)]
                Kg = [sb("dK%d" % g, [128, S_len], BF16, ps) for g in range(3)]
                R_Qg = [Res() for _ in range(3)]
                R_Kg = [Res() for _ in range(3)]
                Vd = sb("dV", [128, NB, 6, 128], BF16, ps)
                R_V = Res()
                S.op("pool", lambda e: e.memset(Vd[:], 1.0), writes=[R_V])
                md = [sb("md0", [128, 5, 512], BF16, ps), sb("md1", [128, 8, 512], BF16, ps), sb("md2", [128, 9, 512], BF16, ps)]
                R_md = Res()
                for g, nm in enumerate(("m_d0", "m_d1", "m_d2")):
                    S.dma("sp", md[g][:], cd[nm], writes=[R_md])
                with contextlib.ExitStack() as pp:
                    wd = sb("wdil", [128, 8, 1152], BF16, pp)
                    wsw = sb("wdsw", [128, 8, 768], BF16, pp)
                    Rw = Res()
                    wload(wd, mix_in_d[l], C_DQ, 1152, Rw)
                    wload(wsw, mix_sw_d[l], 7 * 64, 768, Rw)
                    pw = ProjWork(pp)
                    for t in range(T):
                        b = pw.load(t)
                        for g in range(3):
                            pw.roped(b, lambda kc, g=g: wd[:, kc, g * 128:(g + 1) * 128],
                                     lambda kc, g=g: wsw[:, kc, g * 128:(g + 1) * 128], Rw,
                                     Qg[g][:, t * 512:(t + 1) * 512], R_Qg[g], 0.125)
                            pw.roped(b, lambda kc, g=g: wd[:, kc, 384 + g * 128:384 + (g + 1) * 128],
                                     lambda kc, g=g: wsw[:, kc, 384 + g * 128:384 + (g + 1) * 128], Rw,
                                     Kg[g][:, t * 512:(t + 1) * 512], R_Kg[g], 1.0)
                        for a in range(4):
                            bk = 5 + (a % 2)
                            mm_group(S, banks[bk][:, 0:384], R_bank[bk],
                                     [(pw.nt[b][:, kc, a * 128:(a + 1) * 128], wd[:, kc, 768:1152], [pw.Rn[b], Rw])
                                      for kc in range(8)])
                            S.op("act", lambda e, a=a, bk=bk, t=t: e.activation(
                                out=Vd[:, t * 4 + a, :, 0:64], in_=banks[bk][:, 0:384].rearrange("p (h d) -> p h d", d=64),
                                func=AF.Copy), reads=[R_bank[bk]], writes=[R_V])
                    S.barrier()
                with contextlib.ExitStack() as pa:
                    st = AttnState(pa)
                    units = []
                    for i in range(T):
                        for hh in range(2):
                            ob = 3 + (st.oi % 2)
                            st.oi += 1
                            ul = []
                            for g, rlo in enumerate((-1, -4, -16)):
                                for r in range(rlo, 4):
                                    kb = 4 * i + r
                                    if kb >= 0:
                                        mi = r - rlo
                                        if g == 2:
                                            mi = (r + 16) if r < -12 else (4 if r < 0 else 5 + r)
                                        ul.append((g, kb, md[g][:, mi, :]))
                            pr = slice(64 * hh, 64 * hh + 64)

                            def fin(i=i, ob=ob, hh=hh):
                                finalize_to_dram(st, ob, yT_d[256 + 64 * hh:320 + 64 * hh, i * 512:(i + 1) * 512])
                            for ui, (g, kb, m_ap) in enumerate(ul):
                                units.append(dict(k=Kg[g][pr, kb * 128:(kb + 1) * 128], q=Qg[g][pr, i * 512:(i + 1) * 512],
                                                  rdk=[R_Kg[g], R_Qg[g]], masks=[(ident_bf[:], m_ap, [R_const, R_md])],
                                                  bias=None, rdb=[], v=Vd[:, kb, 2 * g + hh, :], rdv=[R_V], ob=ob,
                                                  first=(ui == 0), last=(ui == len(ul) - 1), fin=fin))
                    run_units(st, units)
                    S.barrier()

        def mixer_nsa(l, mcaus, R_mc):
            NSEL = S_len // 64
            NC = (S_len - 32) // 16 + 1
            NCW = 4 * NSEL
            NCH = (NC + 127) // 128
            with contextlib.ExitStack() as ps:
                Qn = sb("nQ", [128, 2, S_len], BF16, ps)
                Kc2 = sb("nKc", [128, S_len], BF16, ps)
                Ks2 = sb("nKs", [128, S_len], BF16, ps)
                Kw2 = sb("nKw", [128, S_len], BF16, ps)
                Vcf = sb("nVcf", [64, S_len], BF16, ps)
                Vs = sb("nVs", [128, NB, 192], BF16, ps)
                Vw = sb("nVw", [128, NB, 192], BF16, ps)
                GT = sb("nGT", [12, S_len], F32, ps)
                selT = sb("nselT", [128, S_len], BF16, ps)
                KcC = sb("nKcC", [128, max(NCW, 128 * NCH)], BF16, ps)
                VcC = sb("nVcC", [128, 2, 192], BF16, ps)
                R_Qn, R_Kc, R_Ks, R_Kw, R_Vcf, R_Vs, R_Vw, R_GT, R_selT, R_KcC, R_VcC = [Res() for _ in range(11)]
                S.op("pool", lambda e: e.memset(Vs[:], 1.0), writes=[R_Vs])
                S.op("pool", lambda e: e.memset(Vw[:], 1.0), writes=[R_Vw])
                S.op("pool", lambda e: e.memset(VcC[:], 1.0), writes=[R_VcC])
                S.op("pool", lambda e: e.memset(KcC[:], 0.0), writes=[R_KcC])
                R_m = Res()
                with contextlib.ExitStack() as pp:
                    wn = sb("wnsa", [128, 8, 652], BF16, pp)
                    wsw = sb("wnsw", [128, 8, 448], BF16, pp)
                    wdup = sb("wndup", [128, 8, 6, 128], BF16, pp)
                    Rw = Res()
                    wload(wn, mix_in_d[l], 0, 652, Rw)
                    wload(wsw, mix_sw_d[l], 0, 448, Rw)
                    for j, c0 in enumerate((C_KC, C_KS, C_KW)):
                        for half in range(2):
                            S.op("pool", lambda e, j=j, c0=c0, half=half: e.tensor_copy(
                                out=wdup[:, :, 2 * j, half * 64:(half + 1) * 64], in_=wn[:, :, c0:c0 + 64]),
                                reads=[Rw], writes=[Rw])
                            S.op("pool", lambda e, j=j, half=half: e.tensor_copy(
                                out=wdup[:, :, 2 * j + 1, half * 64:(half + 1) * 64], in_=wsw[:, :, (4 + j) * 64:(5 + j) * 64]),
                                reads=[Rw], writes=[Rw])
                    pw = ProjWork(pp)
                    for t in range(T):
                        b = pw.load(t)
                        ts_ = slice(t * 512, (t + 1) * 512)
                        for c in range(2):
                            pw.roped(b, lambda kc, c=c: wn[:, kc, c * 128:(c + 1) * 128],
                                     lambda kc, c=c: wsw[:, kc, c * 128:(c + 1) * 128], Rw, Qn[:, c, ts_], R_Qn, 0.125)
                        for j, (dst, Rd) in enumerate(((Kc2, R_Kc), (Ks2, R_Ks), (Kw2, R_Kw))):
                            pw.roped(b, lambda kc, j=j: wdup[:, kc, 2 * j, :], lambda kc, j=j: wdup[:, kc, 2 * j + 1, :], Rw,
                                     dst[:, ts_], Rd, 1.0)
                        mm_group(S, banks[5][0:64, :], R_bank[5],
                                 [(wn[:, kc, C_VC:C_VC + 64], pw.nt[b][:, kc, :], [Rw, pw.Rn[b]]) for kc in range(8)])
                        S.op("act", lambda e, ts_=ts_: e.activation(out=Vcf[0:64, ts_], in_=banks[5][0:64, :], func=AF.Copy),
                             reads=[R_bank[5]], writes=[R_Vcf])
                        mm_group(S, banks[6][0:12, :], R_bank[6],
                                 [(wn[:, kc, C_NG:C_NG + 12], pw.nt[b][:, kc, :], [Rw, pw.Rn[b]]) for kc in range(8)])
                        S.op("act", lambda e, ts_=ts_: e.activation(out=GT[0:12, ts_], in_=banks[6][0:12, :], func=AF.Sigmoid),
                             reads=[R_bank[6]], writes=[R_GT])
                        for a2 in range(2):
                            bk = 5 + a2
                            for aa in range(2):
                                a = 2 * a2 + aa
                                mm_group(S, banks[bk][:, aa * 192:(aa + 1) * 192], R_bank[bk],
                                         [(pw.nt[b][:, kc, a * 128:(a + 1) * 128], wn[:, kc, C_VS:C_VS + 192], [pw.Rn[b], Rw])
                                          for kc in range(8)])
                            v3 = banks[bk][:, 0:384].rearrange("p (a x) -> p a x", x=192)
                            S.op("dve", lambda e, a2=a2, t=t, v3=v3: e.tensor_copy(
                                out=Vs[:, t * 4 + 2 * a2:t * 4 + 2 * a2 + 2, 64:128], in_=v3[:, :, 0:64]),
                                reads=[R_bank[bk]], writes=[R_Vs])
                            S.op("dve", lambda e, a2=a2, t=t, v3=v3: e.tensor_copy(
                                out=Vw[:, t * 4 + 2 * a2:t * 4 + 2 * a2 + 2, 64:128], in_=v3[:, :, 128:192]),
                                reads=[R_bank[bk]], writes=[R_Vw])
                    S.barrier()
                if nsa_stop == "proj":
                    return
                with contextlib.ExitStack() as pc:
                    w1 = sb("w1", [64, 2, 32, 128], BF16, pc)
                    w2kd = sb("w2kd", [128, 128], BF16, pc)
                    w2v = sb("w2v", [128, 64], BF16, pc)
                    peT = sb("peT", [64, 2, 32], BF16, pc)
                    hidk = sb("hidk", [128, NCW], BF16, pc)
                    hidv = sb("hidv", [128, NCW], BF16, pc)
                    cb = sb("cb", [128, 2], F32, pc)
                    Rc = Res()
                    R_hid = Res()
                    for j in range(2):
                        S.dma("pool", w1[:, j, :, :], w1_d[l, j].rearrange("(l d) h -> d l h", d=64), writes=[Rc])
                        S.dma("pool", peT[:, j, :], pe_d[l, j], writes=[Rc])
                    S.dma("pool", w2kd[:, 0:64], w2_d[l, 0], writes=[Rc])
                    S.dma("pool", w2kd[:, 64:128], w2_d[l, 0], writes=[Rc])
                    S.dma("pool", w2v[:], w2_d[l, 1], writes=[Rc])
                    S.op("pool", lambda e: e.memset(hidk[:], 0.0), writes=[R_hid])
                    S.op("pool", lambda e: e.memset(hidv[:], 0.0), writes=[R_hid])
                    for j, (X, RX, hid) in enumerate(((Kc2, R_Kc, hidk), (Vcf, R_Vcf, hidv))):
                        mm_group(S, banks[5][:, 0:NC], R_bank[5],
                                 [(w1[:, j, li, :], X[0:64, li:li + 16 * (NC - 1) + 1:16], [Rc, RX]) for li in range(32)])
                        mm_group(S, banks[6][:, 0:1], R_bank[6],
                                 [(w1[:, j, li, :], peT[:, j, li:li + 1], [Rc]) for li in range(32)])
                        S.op("dve", lambda e, j=j: e.tensor_copy(out=cb[:, j:j + 1], in_=banks[6][:, 0:1]),
                             reads=[R_bank[6]], writes=[Rc])
                        S.op("act", lambda e, j=j, hid=hid: e.activation(out=hid[:, 0:NC], in_=banks[5][:, 0:NC], func=AF.Silu,
                                                                          bias=cb[:, j:j + 1], scale=1.0),
                             reads=[R_bank[5], Rc], writes=[R_hid])
                    mm_group(S, banks[5][:, 0:NC], R_bank[5], [(w2kd[:], hidk[:, 0:NC], [Rc, R_hid])])
                    S.op("dve", lambda e: e.tensor_copy(out=KcC[:, 0:NC], in_=banks[5][:, 0:NC]), reads=[R_bank[5]],
                         writes=[R_KcC])
                    for cc in range(NCH):
                        rows = min(128, NC - cc * 128)
                        mm_group(S, banks[6][0:rows, 0:64], R_bank[6],
                                 [(hidv[:, cc * 128:cc * 128 + rows], w2v[:], [R_hid, Rc])])
                        S.op("dve", lambda e, cc=cc, rows=rows: e.tensor_copy(out=VcC[0:rows, cc, 64:128],
                                                                              in_=banks[6][0:rows, 0:64]),
                             reads=[R_bank[6]], writes=[R_VcC])
                    S.barrier()
                if nsa_stop == "cmp":
                    return
                with contextlib.ExitStack() as pi:
                    mcQ = sb("mcQ", [128, NB, NCW], BF16, pi)
                    for b0 in range(0, NB, 4):
                        S.dma("sp", mcQ[:, b0:b0 + 4, :], cd["m_cmpQ"][:, b0:b0 + 4, :], writes=[R_m])
                    ee = [sb("ee%d" % i, [128, NCW], F32, pi) for i in range(2)]
                    R_ee = [Res(), Res()]
                    acc = sb("iacc", [128, NCW], F32, pi)
                    R_acc = Res()
                    den = sb("iden", [128, 8], F32, pi)
                    R_den = Res()
                    imp = sb("imp", [128, NSEL], F32, pi)
                    impA = sb("impA", [128, NSEL], F32, pi)
                    impB = sb("impB", [128, NSEL], F32, pi)
                    work = sb("iwork", [128, NSEL], F32, pi)
                    m8 = sb("m8", [128, 16], F32, pi)
                    selb = sb("selb", [128, 128], BF16, pi)
                    R_imp, R_impA, R_impB, R_work, R_m8, R_selb = [Res() for _ in range(6)]
                    accv = acc[:].rearrange("p (j f) -> p j f", f=4)
                    S.op("pool", lambda e: e.memset(selb[:], 0.0), writes=[R_selb])
                    import os as _os
                    _stage = int(_os.environ.get("IMP_STAGE", "9"))
                    _b0 = int(_os.environ.get("IMP_B0", "0"))
                    _b1 = int(_os.environ.get("IMP_B1", str(NB)))
                    for b in range(_b0, _b1):
                        qs = slice(b * 128, (b + 1) * 128)
                        for h in range(4):
                            c = h // 2
                            pr = slice(64 * (h % 2), 64 * (h % 2) + 64)
                            bk = 5 + (h % 2)
                            mm_group(S, banks[bk][:, 0:NCW], R_bank[bk],
                                     [(Qn[pr, c, qs], KcC[pr, 0:NCW], [R_Qn, R_KcC]), (ident_bf[:], mcQ[:, b, :], [R_const, R_m])])
                            S.op("act", lambda e, h=h, bk=bk: e.activation(out=ee[h % 2][:], in_=banks[bk][:, 0:NCW], func=AF.Exp),
                                 reads=[R_bank[bk]], writes=[R_ee[h % 2]])
                            S.op("dve", lambda e, h=h: e.reduce_sum(out=den[:, h:h + 1], in_=ee[h % 2][:], axis=AX.X),
                                 reads=[R_ee[h % 2]], writes=[R_den])
                            S.op("dve", lambda e, h=h: e.tensor_scalar(out=den[:, h:h + 1], in0=den[:, h:h + 1], scalar1=1e-30,
                                                                       scalar2=None, op0=ALU.max), reads=[R_den], writes=[R_den])
                            S.op("dve", lambda e, h=h: e.reciprocal(out=den[:, 4 + h:5 + h], in_=den[:, h:h + 1]),
                                 reads=[R_den], writes=[R_den])
                            if h == 0:
                                S.op("dve", lambda e, h=h: e.tensor_scalar(out=acc[:], in0=ee[0][:], scalar1=den[:, 4:5],
                                                                           scalar2=None, op0=ALU.mult),
                                     reads=[R_ee[0], R_den], writes=[R_acc])
                            else:
                                S.op("dve", lambda e, h=h: e.scalar_tensor_tensor(out=acc[:], in0=ee[h % 2][:],
                                                                                  scalar=den[:, 4 + h:5 + h], in1=acc[:],
                                                                                  op0=ALU.mult, op1=ALU.add),
                                     reads=[R_ee[h % 2], R_den, R_acc], writes=[R_acc])
                        if _stage < 2:
                            continue
                        S.op("dve", lambda e: e.tensor_tensor(out=imp[:], in0=accv[:, :, 0], in1=accv[:, :, 1], op=ALU.add),
                             reads=[R_acc], writes=[R_imp])
                        S.op("dve", lambda e: e.tensor_tensor(out=imp[:], in0=imp[:], in1=accv[:, :, 2], op=ALU.add),
                             reads=[R_acc, R_imp], writes=[R_imp])
                        S.op("dve", lambda e: e.scalar_tensor_tensor(out=imp[:], in0=accv[:, :, 3], scalar=0.5, in1=imp[:],
                                                                     op0=ALU.mult, op1=ALU.add),
                             reads=[R_acc, R_imp], writes=[R_imp])
                        S.op("dve", lambda e: e.scalar_tensor_tensor(out=imp[:, 1:NSEL], in0=accv[:, 0:NSEL - 1, 3], scalar=0.5,
                                                                     in1=imp[:, 1:NSEL], op0=ALU.mult, op1=ALU.add),
                             reads=[R_acc, R_imp], writes=[R_imp])
                        if _stage < 3:
                            continue
                        S.op("pool", lambda e, b=b: e.affine_select(out=impA[:], in_=imp[:], pattern=[[-64, NSEL]],
                                                                    compare_op=ALU.is_ge, fill=fill_big, base=128 * b - 64,
                                                                    channel_multiplier=1),
                             reads=[R_imp], writes=[R_impA])
                        S.op("pool", lambda e, b=b: e.affine_select(out=impB[:], in_=impA[:], pattern=[[-64, NSEL]],
                                                                    compare_op=ALU.is_ge, fill=fill_neg1, base=128 * b,
                                                                    channel_multiplier=1),
                             reads=[R_impA], writes=[R_impB])
                        S.op("pool", lambda e: e.memset(impB[:, 0:1], 1e4), reads=[], writes=[R_impB])
                        if _stage < 4:
                            continue
                        S.op("dve", lambda e: e.max(out=m8[:, 0:8], in_=impB[:]), reads=[R_impB], writes=[R_m8])
                        S.op("dve", lambda e: e.match_replace(out=work[:], in_to_replace=m8[:, 0:8], in_values=impB[:],
                                                              imm_value=-1e9), reads=[R_impB, R_m8], writes=[R_work])
                        S.op("dve", lambda e: e.max(out=m8[:, 8:16], in_=work[:]), reads=[R_work], writes=[R_m8])
                        S.op("dve", lambda e: e.tensor_scalar(out=selb[:, 0:NSEL], in0=impB[:], scalar1=m8[:, 15:16], scalar2=1.0,
                                                              op0=ALU.is_ge, op1=ALU.subtract),
                             reads=[R_impB, R_m8], writes=[R_selb])
                        if _stage < 5:
                            continue
                        S.op("pe", lambda e: e.transpose(out=bankT[:, 0:128], in_=selb[:], identity=ident_bf[:]),
                             reads=[R_selb, R_const], writes=[R_bankT])
                        S.op("act", lambda e, qs=qs: e.activation(out=selT[:, qs], in_=bankT[:, 0:128], func=AF.Copy),
                             reads=[R_bankT], writes=[R_selT])
                    S.barrier()
                if nsa_stop == "imp":
                    return
                with contextlib.ExitStack() as pa:
                    mwin = sb("mwin", [128, 8, 512], BF16, pa)
                    mcT = sb("mcT", [128, T * 2, 512], BF16, pa)
                    esel = sb("esel", [128, NB, 128], BF16, pa)
                    gsel = sb("gsel", [12, 6, 128], F32, pa)
                    S.dma("sp", mwin[:], cd["m_win"], writes=[R_m])
                    for i0 in range(0, 2 * T, 4):
                        S.dma("sp", mcT[:, i0:i0 + 4, :], cd["m_cmpT"][:, i0:i0 + 4, :], writes=[R_m])
                    S.dma("sp", esel[:], cd["e_sel"], writes=[R_m])
                    S.dma("sp", gsel[:], cd["g_sel"], writes=[R_m])
                    st = AttnState(pa)
                    accs = [sb("nacc%d" % i, [128, 512], F32, pa) for i in range(2)]
                    R_accs = [Res(), Res()]
                    wt = [sb("nw%d" % i, [128, 512], F32, pa) for i in range(2)]
                    R_wt = [Res(), Res()]
                    tmpo = [sb("nto%d" % i, [128, 512], F32, pa) for i in range(2)]
                    R_tmpo = [Res(), Res()]
                    units = []
                    gi = [0]
                    for c in range(2):
                        for i in range(int(_os.environ.get("NSA_I0", "0")), int(_os.environ.get("NSA_I1", str(T)))):
                            ai = (c * T + i) % 2
                            qsl = slice(i * 512, (i + 1) * 512)
                            for br in [int(x) for x in _os.environ.get("NSA_BRS", "0,1,2").split(",")]:
                                gb = 5 + (gi[0] % 2)
                                gi[0] += 1
                                for hh in range(2):
                                    ob = 3 + (st.oi % 2)
                                    st.oi += 1
                                    pr = slice(64 * hh, 64 * hh + 64)
                                    variant = "A" if hh == 0 else "B"
                                    vcol = slice(64, 192) if hh == 0 else slice(0, 128)
                                    ul = []
                                    if br == 0:
                                        for cc in range(NCH):
                                            if 16 * (128 * cc) + 31 > 512 * i + 511:
                                                continue
                                            ul.append(dict(k=KcC[pr, cc * 128:(cc + 1) * 128], rdk=[R_KcC, R_Qn],
                                                           masks=[(ident_bf[:], mcT[:, 2 * i + cc, :], [R_const, R_m])],
                                                           v=VcC[:, cc, vcol], rdv=[R_VcC]))
                                    elif br == 1:
                                        for kb in range(4 * i + 4):
                                            masks = [(esel[:, kb, :], selT[:, qsl], [R_m, R_selT])]
                                            if kb >= 4 * i:
                                                masks.append((ident_bf[:], mcaus[:, kb - 4 * i, :], [R_const, R_mc]))
                                            ul.append(dict(k=Ks2[pr, kb * 128:(kb + 1) * 128], rdk=[R_Ks, R_Qn], masks=masks,
                                                           v=Vs[:, kb, vcol], rdv=[R_Vs]))
                                    else:
                                        for r in range(-4, 4):
                                            kb = 4 * i + r
                                            if kb < 0:
                                                continue
                                            ul.append(dict(k=Kw2[pr, kb * 128:(kb + 1) * 128], rdk=[R_Kw, R_Qn],
                                                           masks=[(ident_bf[:], mwin[:, r + 4, :], [R_const, R_m])],
                                                           v=Vw[:, kb, vcol], rdv=[R_Vw]))

                                    def fin(c=c, i=i, br=br, hh=hh, ob=ob, gb=gb, ai=ai, qsl=qsl, variant=variant):
                                        if hh == 0:
                                            mm_group(S, banks[gb][:], R_bank[gb],
                                                     [(gsel[:, c * 3 + br, :], GT[0:12, qsl], [R_m, R_GT])])
                                        nr, rd, ri = std_finalize(st, ob, variant, None, None, clamp=(br == 0))
                                        S.op("dve", lambda e: e.tensor_tensor(out=wt[hh][nr, :], in0=rd[nr, :], in1=banks[gb][nr, :],
                                                                              op=ALU.mult),
                                             reads=[st.Rrd[ri], R_bank[gb]], writes=[R_wt[hh]])
                                        if br == 0:
                                            S.op("dve", lambda e: e.tensor_tensor(out=accs[ai][nr, :], in0=banks[ob][nr, :],
                                                                                  in1=wt[hh][nr, :], op=ALU.mult),
                                                 reads=[R_bank[ob], R_wt[hh]], writes=[R_accs[ai]])
                                        else:
                                            S.op("dve", lambda e: e.tensor_tensor(out=tmpo[hh][nr, :], in0=banks[ob][nr, :],
                                                                                  in1=wt[hh][nr, :], op=ALU.mult),
                                                 reads=[R_bank[ob], R_wt[hh]], writes=[R_tmpo[hh]])
                                            S.op("pool", lambda e: e.tensor_tensor(out=accs[ai][nr, :], in0=accs[ai][nr, :],
                                                                                   in1=tmpo[hh][nr, :], op=ALU.add),
                                                 reads=[R_tmpo[hh], R_accs[ai]], writes=[R_accs[ai]])
                                        if br == 2 and hh == 1 and not _os.environ.get("NSA_NOFINAL"):
                                            ri2 = st.ri % 2
                                            st.ri += 1
                                            S.op("act", lambda e: e.activation(out=st.ys[ri2][:], in_=accs[ai][:], func=AF.Copy),
                                                 reads=[R_accs[ai]], writes=[st.Rys[ri2]])
                                            S.dma(_os.environ.get("NSA_FQ", "pool"), yT_d[c * 128:(c + 1) * 128, qsl], st.ys[ri2][:],
                                                  reads=[st.Rys[ri2]], writes=[R_y])
                                    if len(ul) == 0:
                                        raise AssertionError("empty unit list")
                                    for ui, u in enumerate(ul):
                                        u.update(q=Qn[pr, c, qsl], bias=None, rdb=[], ob=ob, first=(ui == 0),
                                                 last=(ui == len(ul) - 1), fin=fin)
                                        units.append(u)
                    run_units(st, units)
                    S.barrier()

        def mixer_merge(l):
            with contextlib.ExitStack() as ps:
                wmg = sb("wmg", [128, 8, 3072], BF16, ps)
                wbr = sb("wbr", [128, 6, D], BF16, ps)
                wmo = sb("wmo", [128, 8, D], BF16, ps)
                Rw = Res()
                wload(wmg, mix_in_d[l], C_MG, 3072, Rw)
                for cch in range(2):
                    S.dma("pool", wbr[:, cch, :], brn_d[l, cch * 128:(cch + 1) * 128, :], writes=[Rw])
                S.dma("pool", wbr[:, 2, :], brd_d[l], writes=[Rw])
                for cch in range(3):
                    S.dma("pool", wbr[:, 3 + cch, :], brf_d[l, cch * 128:(cch + 1) * 128, :], writes=[Rw])
                for kc in range(8):
                    S.dma("pool", wmo[:, kc, :], mo_d[l, kc * 128:(kc + 1) * 128, :], writes=[Rw])
                nt = [sb("mn%d" % i, [128, 8, 512], BF16, ps) for i in range(2)]
                Rn = [Res(), Res()]
                yt = [sb("my%d" % i, [128, 6, 512], BF16, ps) for i in range(2)]
                Ryt = [Res(), Res()]
                mg = sb("mgd", [128, 8, 512], BF16, ps)
                R_mg = [Res() for _ in range(8)]
                sgt = [sb("msg%d" % i, [128, 512], F32, ps) for i in range(2)]
                R_sgt = [Res(), Res()]
                macc = [sb("macc%d" % i, [128, 512], F32, ps) for i in range(2)]
                R_macc = [Res(), Res()]
                hr = [sb("mhr%d" % i, [128, 512], F32, ps) for i in range(2)]
                R_hr = [Res(), Res()]
                hn = [sb("mhn%d" % i, [128, 512], F32, ps) for i in range(2)]
                R_hn = [Res(), Res()]
                brch = ((0, 2), (2, 1), (3, 3))
                k = 0
                for t in range(T):
                    b = t % 2
                    tsl = slice(t * 512, (t + 1) * 512)
                    S.dma("sp", nt[b][:], n2_tile(t), reads=[R_n2[t]], writes=[Rn[b]])
                    S.dma("sp", yt[b][:], yT_d[:, tsl].rearrange("(c p) t -> p c t", p=128), reads=[R_y], writes=[Ryt[b]])
                    for oc in range(8):
                        ma = oc % 2
                        for bi, (c0, ncch) in enumerate(brch):
                            bg = k % 2
                            by = 2 + (k % 2)
                            k += 1
                            mm_group(S, banks[bg][:], R_bank[bg],
                                     [(wmg[:, kc, bi * 1024 + oc * 128:bi * 1024 + (oc + 1) * 128], nt[b][:, kc, :], [Rw, Rn[b]])
                                      for kc in range(8)])
                            mm_group(S, banks[by][:], R_bank[by],
                                     [(wbr[:, c0 + cc, oc * 128:(oc + 1) * 128], yt[b][:, c0 + cc, :], [Rw, Ryt[b]])
                                      for cc in range(ncch)])
                            S.op("act", lambda e, bg=bg: e.activation(out=sgt[bg][:], in_=banks[bg][:], func=AF.Sigmoid),
                                 reads=[R_bank[bg]], writes=[R_sgt[bg]])
                            if bi == 0:
                                S.op("dve", lambda e, bg=bg, by=by, ma=ma: e.tensor_tensor(out=macc[ma][:], in0=sgt[bg][:],
                                                                                          in1=banks[by][:], op=ALU.mult),
                                     reads=[R_sgt[bg], R_bank[by]], writes=[R_macc[ma]])
                            else:
                                S.op("dve", lambda e, bg=bg, by=by: e.tensor_tensor(out=sgt[bg][:], in0=sgt[bg][:],
                                                                                   in1=banks[by][:], op=ALU.mult),
                                     reads=[R_sgt[bg], R_bank[by]], writes=[R_sgt[bg]])
                                if bi == 1:
                                    S.op("pool", lambda e, bg=bg, ma=ma: e.tensor_tensor(out=macc[ma][:], in0=macc[ma][:],
                                                                                        in1=sgt[bg][:], op=ALU.add),
                                         reads=[R_sgt[bg], R_macc[ma]], writes=[R_macc[ma]])
                                else:
                                    S.op("pool", lambda e, bg=bg, ma=ma, oc=oc: e.tensor_tensor(out=mg[:, oc, :], in0=macc[ma][:],
                                                                                               in1=sgt[bg][:], op=ALU.add),
                                         reads=[R_sgt[bg], R_macc[ma]], writes=[R_mg[oc]])
                    for oc in range(8):
                        bd = 4 + (oc % 2)
                        bb = oc % 2
                        S.dma("sp", hr[bb][:], hT_d[oc * 128:(oc + 1) * 128, tsl], reads=[R_hT[t]], writes=[R_hr[bb]])
                        mm_group(S, banks[bd][:], R_bank[bd],
                                 [(wmo[:, kc, oc * 128:(oc + 1) * 128], mg[:, kc, :], [Rw, R_mg[kc]]) for kc in range(8)])
                        S.op("dve", lambda e, oc=oc, bd=bd, bb=bb: e.scalar_tensor_tensor(
                            out=hn[bb][:], in0=banks[bd][:], scalar=hga[:, 1, oc:oc + 1], in1=hr[bb][:],
                            op0=ALU.mult, op1=ALU.add), reads=[R_bank[bd], R_hr[bb], R_mod], writes=[R_hn[bb]])
                        S.dma("sp", hT_d[oc * 128:(oc + 1) * 128, tsl], hn[bb][:], reads=[R_hn[bb]], writes=[R_hT[t]])
                S.barrier()

        phase_transpose_in()
        for l in range(n_layers):
            compute_mod(l)
            phase_ffn(l, 0)
            if stop_after == "ffn1":
                break
            phase_n2()
            phase_mixer(l)
            if stop_after == "mixer":
                break
            phase_ffn(l, 1)
        phase_final()
    return nc


_ROPED_HEAD_COLS = ([C_NQ + 64 * h for h in range(4)] + [C_KC, C_KS, C_KW]
                    + [C_DQ + 64 * h for h in range(6)] + [C_DK + 64 * h for h in range(6)])


def prepare_inputs(inputs, S_len, n_cores, consts):
    f = lambda a: np.ascontiguousarray(np.asarray(a, dtype=np.float32))
    L = inputs["ada_w"].shape[0]
    perm = np.concatenate([np.arange(8, 16), np.arange(0, 8), np.arange(16, 64)])
    swapcols = np.concatenate([c0 + perm for c0 in _ROPED_HEAD_COLS])
    mix_in = f(inputs["mix_w_in"])
    shared = {
        "ada_w": f(inputs["ada_w"]),
        "ada_b": f(np.asarray(inputs["ada_b"]).reshape(L, 72, 128).transpose(0, 2, 1)),
        "norm_g": f(np.asarray(inputs["norm_g"]).reshape(L, 3, 8, 128).transpose(0, 1, 3, 2)),
        "final_norm_g": f(np.asarray(inputs["final_norm_g"]).reshape(8, 128).T),
        "ffn_w_in": f(inputs["ffn_w_in"]),
        "ffn_w_out": f(inputs["ffn_w_out"]),
        "mix_w_in": mix_in,
        "mix_w_swap": f(mix_in[:, :, swapcols]),
        "nsa_cmp_peT": f(np.asarray(inputs["nsa_cmp_pe"]).transpose(0, 1, 3, 2)),
        "nsa_cmp_w1": f(inputs["nsa_cmp_w1"]),
        "nsa_cmp_w2": f(inputs["nsa_cmp_w2"]),
        "fox_f_bias": f(np.broadcast_to(np.asarray(inputs["fox_f_bias"])[:, None, :], (L, 128, 6))),
        "br_w_nsa": f(inputs["br_w_nsa"]),
        "br_w_dil": f(inputs["br_w_dil"]),
        "br_w_fox": f(inputs["br_w_fox"]),
        "mix_w_out": f(inputs["mix_w_out"]),
    }
    shared.update(consts)
    x = np.asarray(inputs["x"], dtype=np.float32)
    c = np.asarray(inputs["c"], dtype=np.float32)
    maps = []
    for b in range(n_cores):
        m = dict(shared)
        m["x"] = np.ascontiguousarray(x[b])
        m["c"] = np.ascontiguousarray(c[b].reshape(8, 128).T)
        maps.append(m)
    return maps


def run(inputs, S_len, n_cores, **bkw):
    consts = host_constants(S_len)
    nc = build_program(S_len, consts, **bkw)
    maps = prepare_inputs(inputs, S_len, n_cores, consts)
    res = run_bass_kernel_spmd(nc, maps, core_ids=list(range(n_cores)))
    if bkw.get("dbg"):
        return res.results
    return np.stack([np.asarray(r["out"], dtype=np.float32) for r in res.results], axis=0)


def kernel(**inputs):
    return run(inputs, 4096, 8)
```
